# Optimizing a Trainium2 kernel written in Bass

```python
import jax, jax.numpy as jnp
from jax import lax
import numpy as np

D_MODEL = 2048
BATCH = 4
SEQ = 2048
DEPTH = 2

CTX_LEN = 256
GRID_W = 64
EPS = 1e-6

GLA_HEADS = 4
GLA_DK = 256
GLA_DV = 512
GLA_QK = GLA_HEADS * GLA_DK
GLA_VW = GLA_HEADS * GLA_DV
GLA_RANK = 16
GLA_TAU = 16.0
GLA_CHUNK = 64

MLA_HEADS = 16
MLA_Q_RANK = 512
MLA_KV_RANK = 512
MLA_NOPE = 128
MLA_ROPE = 64
MLA_V = 128
MLA_QK = MLA_NOPE + MLA_ROPE
MLA_VW = MLA_HEADS * MLA_V
ROPE_BASE = 10000.0
Q_BLOCK = 128

D_FF = -(-8 * D_MODEL // (3 * 256)) * 256

IN_SPLITS = (GLA_QK, GLA_QK, GLA_VW, GLA_VW, GLA_RANK, GLA_RANK,
             MLA_Q_RANK, MLA_KV_RANK, MLA_ROPE, D_MODEL, D_MODEL)
N_IN = sum(IN_SPLITS)
IN_OFFSETS = tuple(int(v) for v in np.cumsum(IN_SPLITS)[:-1])

kernel_name = "hybrid_gla_mla_prefix_dit"


def rmsnorm(x, g):
    xf = x.astype(jnp.float32)
    y = xf * lax.rsqrt(jnp.mean(xf * xf, axis=-1, keepdims=True) + EPS)
    return (y * g.astype(jnp.float32)).astype(x.dtype)


def modulate(h, shift, scale):
    return h * (1.0 + scale) + shift


def axial_rope_tables(n):
    rows = n // GRID_W
    row = jnp.repeat(jnp.arange(rows, dtype=jnp.float32), GRID_W)
    col = jnp.tile(jnp.arange(GRID_W, dtype=jnp.float32), rows)
    n_pairs = MLA_ROPE // 4
    freqs = ROPE_BASE ** (-jnp.arange(n_pairs, dtype=jnp.float32) / n_pairs)
    ang = jnp.concatenate([row[:, None] * freqs, col[:, None] * freqs], axis=-1)
    return jnp.cos(ang), jnp.sin(ang)


def apply_rope(x, cos, sin):
    xp = x.reshape(x.shape[:-1] + (MLA_ROPE // 2, 2))
    x0, x1 = xp[..., 0], xp[..., 1]
    cos = cos.astype(x.dtype)
    sin = sin.astype(x.dtype)
    return jnp.stack([x0 * cos - x1 * sin, x0 * sin + x1 * cos], axis=-1).reshape(x.shape)


def gla_chunked(q, k, v, log_a, s0):
    b, t, h, _ = q.shape
    dv = v.shape[-1]
    nc = t // GLA_CHUNK

    def to_chunks(z):
        z = z.astype(jnp.float32)
        return z.reshape(b, nc, GLA_CHUNK, h, z.shape[-1]).transpose(1, 0, 3, 2, 4)

    qc, kc, vc, gc = map(to_chunks, (q, k, v, log_a))
    gcum = jnp.cumsum(gc, axis=-2)
    mask = jnp.tril(jnp.ones((GLA_CHUNK, GLA_CHUNK), dtype=bool))

    def step(s, inp):
        qi, ki, vi, gi = inp
        g_last = gi[:, :, -1, :]
        q_dec = qi * jnp.exp(gi)
        k_dec = ki * jnp.exp(-gi)
        att = jnp.where(mask, jnp.einsum('bhid,bhjd->bhij', q_dec, k_dec), 0.0)
        o = jnp.einsum('bhij,bhjv->bhiv', att, vi) + jnp.einsum('bhid,bhdv->bhiv', q_dec, s)
        k_to_end = ki * jnp.exp(g_last[:, :, None, :] - gi)
        s_new = s * jnp.exp(g_last)[..., None] + jnp.einsum('bhjd,bhjv->bhdv', k_to_end, vi)
        return s_new, o

    s_fin, oc = lax.scan(step, s0, (qc, kc, vc, gcum))
    o = oc.transpose(1, 0, 3, 2, 4).reshape(b, t, h, dv)
    return o.astype(v.dtype), s_fin


def gla_bidirectional(q, k, v, la_f, la_b, s0_f, s0_b):
    o_f, s_f = gla_chunked(q, k, v, la_f, s0_f)
    o_b, s_b = gla_chunked(q[:, ::-1], k[:, ::-1], v[:, ::-1], la_b[:, ::-1], s0_b)
    return o_f + o_b[:, ::-1], s_f, s_b


def gla_features(q, k, v, gf, gb, w_up_f, b_f, w_up_b, b_b):
    b, t, _ = q.shape
    shp_k = (b, t, GLA_HEADS, GLA_DK)
    q = q.reshape(shp_k) * (GLA_DK ** -0.5)
    k = k.reshape(shp_k)
    v = v.reshape(b, t, GLA_HEADS, GLA_DV)
    la_f = jax.nn.log_sigmoid((gf @ w_up_f + b_f).astype(jnp.float32)) / GLA_TAU
    la_b = jax.nn.log_sigmoid((gb @ w_up_b + b_b).astype(jnp.float32)) / GLA_TAU
    return q, k, v, la_f.reshape(shp_k), la_b.reshape(shp_k)


def gla_output(o, r, g):
    b, t = o.shape[:2]
    return rmsnorm(o, g).reshape(b, t, GLA_VW) * jax.nn.silu(r)


def mla_features(cq, ckv, krope, g_q, w_q_up, g_kv, w_kv_up, cos=None, sin=None):
    b, t, _ = cq.shape
    q = (rmsnorm(cq, g_q) @ w_q_up).reshape(b, t, MLA_HEADS, MLA_QK)
    kv = (rmsnorm(ckv, g_kv) @ w_kv_up).reshape(b, t, MLA_HEADS, MLA_NOPE + MLA_V)
    q_nope, q_rope = q[..., :MLA_NOPE], q[..., MLA_NOPE:]
    k_nope, v = kv[..., :MLA_NOPE], kv[..., MLA_NOPE:]
    if cos is not None:
        q_rope = apply_rope(q_rope, cos[:, None, :], sin[:, None, :])
        krope = apply_rope(krope, cos, sin)
    return q_nope, q_rope, k_nope, krope, v


def mla_attend(q_nope, q_rope, k_nope, k_rope, v):
    b, n, h, _ = q_nope.shape
    nb = n // Q_BLOCK
    scale = MLA_QK ** -0.5
    qn = q_nope.reshape(b, nb, Q_BLOCK, h, MLA_NOPE).transpose(1, 0, 2, 3, 4)
    qr = q_rope.reshape(b, nb, Q_BLOCK, h, MLA_ROPE).transpose(1, 0, 2, 3, 4)

    def one_block(args):
        qn_b, qr_b = args
        s = (jnp.einsum('bqhd,bkhd->bhqk', qn_b, k_nope)
             + jnp.einsum('bqhr,bkr->bhqk', qr_b, k_rope)).astype(jnp.float32) * scale
        p = jax.nn.softmax(s, axis=-1).astype(v.dtype)
        return jnp.einsum('bhqk,bkhv->bqhv', p, v)

    o = lax.map(one_block, (qn, qr))
    return o.transpose(1, 0, 2, 3, 4).reshape(b, n, h * MLA_V)


def merge_branches(gla, mla, gate_a, gate_b, w_a, w_b, w_o):
    y = jax.nn.sigmoid(gate_a) * (gla @ w_a) + jax.nn.sigmoid(gate_b) * (mla @ w_b)
    return y @ w_o


def swiglu(h, w_gate, w_up, w_down):
    return (jax.nn.silu(h @ w_gate) * (h @ w_up)) @ w_down


def setup_inputs(seed: int = 0) -> dict:
    key = jax.random.key(seed)
    ks = jax.random.split(key, 32)
    f32 = jnp.float32
    L = DEPTH

    def nrm(k, shape, scale):
        return jax.random.normal(k, shape, f32) * scale

    def gain(k, shape):
        return 1.0 + 0.05 * jax.random.normal(k, shape, f32)

    return {
        "x": nrm(ks[0], (BATCH, SEQ, D_MODEL), 1.0),
        "c": nrm(ks[1], (BATCH, D_MODEL), 1.0),
        "ctx": nrm(ks[2], (BATCH, CTX_LEN, D_MODEL), 1.0),
        "c_ctx": nrm(ks[3], (D_MODEL,), 1.0),
        "w_mod": nrm(ks[4], (L, D_MODEL, 6 * D_MODEL), 0.5 * D_MODEL ** -0.5),
        "b_mod": nrm(ks[5], (L, 6 * D_MODEL), 0.02),
        "g_attn": gain(ks[6], (L, D_MODEL)),
        "g_ffn": gain(ks[7], (L, D_MODEL)),
        "w_in": nrm(ks[8], (L, D_MODEL, N_IN), D_MODEL ** -0.5),
        "w_gla_up_f": nrm(ks[9], (L, GLA_RANK, GLA_QK), GLA_RANK ** -0.5),
        "b_gla_f": nrm(ks[10], (L, GLA_QK), 0.1),
        "w_gla_up_b": nrm(ks[11], (L, GLA_RANK, GLA_QK), GLA_RANK ** -0.5),
        "b_gla_b": nrm(ks[12], (L, GLA_QK), 0.1),
        "g_gla_out": gain(ks[13], (L, GLA_DV)),
        "g_q_lora": gain(ks[14], (L, MLA_Q_RANK)),
        "w_q_up": nrm(ks[15], (L, MLA_Q_RANK, MLA_HEADS * MLA_QK), MLA_Q_RANK ** -0.5),
        "g_kv_lora": gain(ks[16], (L, MLA_KV_RANK)),
        "w_kv_up": nrm(ks[17], (L, MLA_KV_RANK, MLA_HEADS * (MLA_NOPE + MLA_V)), MLA_KV_RANK ** -0.5),
        "w_branch_a": nrm(ks[18], (L, GLA_VW, D_MODEL), GLA_VW ** -0.5),
        "w_branch_b": nrm(ks[19], (L, MLA_VW, D_MODEL), MLA_VW ** -0.5),
        "w_out": nrm(ks[20], (L, D_MODEL, D_MODEL), D_MODEL ** -0.5),
        "w_ffn_gate": nrm(ks[21], (L, D_MODEL, D_FF), D_MODEL ** -0.5),
        "w_ffn_up": nrm(ks[22], (L, D_MODEL, D_FF), D_MODEL ** -0.5),
        "w_ffn_down": nrm(ks[23], (L, D_FF, D_MODEL), D_FF ** -0.5),
        "g_final": gain(ks[24], (D_MODEL,)),
    }


def reference(x, c, ctx, c_ctx, w_mod, b_mod, g_attn, g_ffn, w_in,
              w_gla_up_f, b_gla_f, w_gla_up_b, b_gla_b, g_gla_out,
              g_q_lora, w_q_up, g_kv_lora, w_kv_up,
              w_branch_a, w_branch_b, w_out,
              w_ffn_gate, w_ffn_up, w_ffn_down, g_final):
    b, n, _ = x.shape
    cos, sin = axial_rope_tables(n)
    s_c = ctx
    s0 = jnp.zeros((b, GLA_HEADS, GLA_DK, GLA_DV), jnp.float32)

    for l in range(DEPTH):
        last = l == DEPTH - 1
        mod_x = (jax.nn.silu(c) @ w_mod[l] + b_mod[l])[:, None, :]
        mod_c = jax.nn.silu(c_ctx) @ w_mod[l] + b_mod[l]
        sa_x, ca_x, ga_x, sf_x, cf_x, gf_x = jnp.split(mod_x, 6, axis=-1)
        sa_c, ca_c, ga_c, sf_c, cf_c, gf_c = jnp.split(mod_c, 6, axis=-1)

        hx = modulate(rmsnorm(x, g_attn[l]), sa_x, ca_x)
        hc = modulate(rmsnorm(s_c, g_attn[l]), sa_c, ca_c)
        (qx, kx, vx, rx, gfx, gbx, cqx, ckvx, krx, bgax, bgbx) = jnp.split(hx @ w_in[l], IN_OFFSETS, axis=-1)
        (qc, kc, vc, rc, gfc, gbc, cqc, ckvc, krc, bgac, bgbc) = jnp.split(hc @ w_in[l], IN_OFFSETS, axis=-1)

        feats_c = gla_features(qc, kc, vc, gfc, gbc, w_gla_up_f[l], b_gla_f[l], w_gla_up_b[l], b_gla_b[l])
        feats_x = gla_features(qx, kx, vx, gfx, gbx, w_gla_up_f[l], b_gla_f[l], w_gla_up_b[l], b_gla_b[l])
        o_c, st_f, st_b = gla_bidirectional(*feats_c, s0, s0)
        o_x, _, _ = gla_bidirectional(*feats_x, st_f, st_b)
        gla_x = gla_output(o_x, rx, g_gla_out[l])

        qn_c, qr_c, kn_c, kr_c, v_c = mla_features(cqc, ckvc, krc, g_q_lora[l], w_q_up[l], g_kv_lora[l], w_kv_up[l])
        qn_x, qr_x, kn_x, kr_x, v_x = mla_features(cqx, ckvx, krx, g_q_lora[l], w_q_up[l], g_kv_lora[l], w_kv_up[l], cos, sin)
        mla_x = mla_attend(qn_x, qr_x,
                           jnp.concatenate([kn_c, kn_x], axis=1),
                           jnp.concatenate([kr_c, kr_x], axis=1),
                           jnp.concatenate([v_c, v_x], axis=1))

        out_x = merge_branches(gla_x, mla_x, bgax, bgbx, w_branch_a[l], w_branch_b[l], w_out[l])
        x = x + ga_x * out_x
        if not last:
            gla_c = gla_output(o_c, rc, g_gla_out[l])
            mla_c = mla_attend(qn_c, qr_c, kn_c, kr_c, v_c)
            out_c = merge_branches(gla_c, mla_c, bgac, bgbc, w_branch_a[l], w_branch_b[l], w_out[l])
            s_c = s_c + ga_c * out_c

        hx = modulate(rmsnorm(x, g_ffn[l]), sf_x, cf_x)
        x = x + gf_x * swiglu(hx, w_ffn_gate[l], w_ffn_up[l], w_ffn_down[l])
        if not last:
            hc = modulate(rmsnorm(s_c, g_ffn[l]), sf_c, cf_c)
            s_c = s_c + gf_c * swiglu(hc, w_ffn_gate[l], w_ffn_up[l], w_ffn_down[l])

    return rmsnorm(x, g_final)
```

```python
import numpy as np
import concourse.bass as bass
import concourse.mybir as mybir
from concourse.bass_utils import run_bass_kernel_spmd

F32 = mybir.dt.float32
BF16 = mybir.dt.bfloat16
AF = mybir.ActivationFunctionType
ALU = mybir.AluOpType
AX = mybir.AxisListType

D = 2048
L = 2
NCTX = 256
NLAT = 1024
T = NCTX + NLAT
KC = D // 128
DFF = 5632
NIN = 11360
OFF_Q, OFF_K, OFF_V, OFF_R, OFF_GF, OFF_GB, OFF_CQ, OFF_CKV, OFF_KR, OFF_BGA, OFF_BGB = (
    0, 1024, 2048, 4096, 6144, 6160, 6176, 6688, 7200, 7264, 9312)
EPS = 1e-6
CH = 64
NCH = T // CH
SCALE_MLA = 192 ** -0.5


class CutExc(Exception):
    pass


class Buf:
    __slots__ = ("w", "r")

    def __init__(self):
        self.w = None
        self.r = {}


ROLL = 30000
NDSEM = 14


class Em:
    def __init__(self, nc):
        self.nc = nc
        self.eng = {"pe": nc.tensor, "act": nc.scalar, "dve": nc.vector, "pool": nc.gpsimd, "sp": nc.sync}
        self.cnt = {k: 0 for k in self.eng}
        self.sems = {k: [] for k in self.eng}
        self.waited = {k: {} for k in self.eng}
        self.dsem = {}
        self.duse = {}
        self.dnext = {"sp": 0, "pool": 0, "act": 0}
        self.same_engine_sync = True
        self.nops = 0
        self.oplim = 0

    def _tick(self):
        self.nops += 1
        if self.oplim and self.nops >= self.oplim:
            self.oplim = 0
            raise CutExc()

    def _esem(self, e, n):
        i = (n - 1) // ROLL
        while len(self.sems[e]) <= i:
            self.sems[e].append(self.nc.alloc_semaphore(name=f"s_{e}_{len(self.sems[e])}"))
        return self.sems[e][i], (n - 1) % ROLL + 1

    def _wait(self, e, tok):
        kind, key, val = tok
        wk = (kind, key)
        if self.waited[e].get(wk, 0) >= val:
            return
        if kind == "e":
            if key == e and (e == "pe" or not self.same_engine_sync):
                return
            sem, v = self._esem(key, val)
            self.eng[e].wait_ge(sem, v)
        else:
            self.eng[e].wait_ge(self.dsem[key], val)
        self.waited[e][wk] = val

    def _deps(self, e, reads, writes):
        toks = []
        for b in reads:
            if b.w is not None:
                toks.append(b.w)
        for b in writes:
            if b.w is not None:
                toks.append(b.w)
            toks.extend(b.r.values())
        for t in toks:
            self._wait(e, t)

    def _mark(self, tok, reads, writes):
        for b in reads:
            k = (tok[0], tok[1])
            b.r[k] = tok
        for b in writes:
            b.w = tok
            b.r = {}

    def op(self, e, fn, reads=(), writes=()):
        self._tick()
        self._deps(e, reads, writes)
        ins = fn(self.eng[e])
        self.cnt[e] += 1
        sem, _ = self._esem(e, self.cnt[e])
        ins.then_inc(sem, 1)
        tok = ("e", e, self.cnt[e])
        self._mark(tok, reads, writes)
        return tok

    def mm(self, ps_ap, pairs, reads, psbuf, transpose=False):
        self._tick()
        self._deps("pe", reads, [psbuf])
        n = len(pairs)
        ins = None
        for i, (a, b) in enumerate(pairs):
            ins = self.nc.tensor.matmul(ps_ap, a, b, start=(i == 0), stop=(i == n - 1))
        self.cnt["pe"] += 1
        sem, _ = self._esem("pe", self.cnt["pe"])
        ins.then_inc(sem, 1)
        tok = ("e", "pe", self.cnt["pe"])
        self._mark(tok, reads, [psbuf])
        return tok

    def tr(self, items, reads, psbuf, ident):
        self._tick()
        self._deps("pe", reads, [psbuf])
        ins = None
        for (o, i) in items:
            ins = self.nc.tensor.transpose(o, i, ident)
        self.cnt["pe"] += 1
        sem, _ = self._esem("pe", self.cnt["pe"])
        ins.then_inc(sem, 1)
        tok = ("e", "pe", self.cnt["pe"])
        self._mark(tok, reads, [psbuf])
        return tok

    def dma(self, q, out, in_, reads=(), writes=()):
        self._tick()
        slot = self.dnext[q] % NDSEM
        self.dnext[q] += 1
        key = (q, slot)
        if key not in self.dsem:
            self.dsem[key] = self.nc.alloc_semaphore(name=f"d_{q}_{slot}")
            self.duse[key] = 0
        if self.duse[key] > 0:
            self._wait(q, ("d", key, 16 * self.duse[key]))
        self._deps(q, reads, writes)
        self.eng[q].dma_start(out=out, in_=in_).then_inc(self.dsem[key], 16)
        self.duse[key] += 1
        tok = ("d", key, 16 * self.duse[key])
        self._mark(tok, reads, writes)
        return tok

    def coll(self, ins, outs, reads=(), writes=(), group=None):
        key = ("cc", len(self.dsem))
        self.dsem[key] = self.nc.alloc_semaphore(name=f"cc_{len(self.dsem)}")
        self._deps("pool", reads, writes)
        self.nc.gpsimd.collective_compute("AllGather", ALU.bypass, replica_groups=[list(range(8))],
                                          ins=[ins], outs=[outs]).then_inc(self.dsem[key], 1)
        self.duse[key] = 1.0 / 16.0
        tok = ("d", key, 1)
        self._mark(tok, reads, writes)
        return tok

    def all_tokens(self):
        toks = []
        for e, n in self.cnt.items():
            if n > 0:
                toks.append(("e", e, n))
        for key, u in self.duse.items():
            if u > 0:
                toks.append(("d", key, int(round(16 * u))))
        return toks

    def barrier(self, engines=None):
        toks = self.all_tokens()
        for e in (engines or self.eng.keys()):
            for t in toks:
                if t[0] == "e" and t[1] == e:
                    continue
                self._wait(e, t)


def tok_chunks(t0, t1, w=512):
    out = []
    t = t0
    while t < t1:
        n = min(w, t1 - t)
        out.append((t, n))
        t += n
    return out


_CUT = {"n": 0, "lim": 0}


def step():
    _CUT["n"] += 1
    if _CUT["lim"] and _CUT["n"] >= _CUT["lim"]:
        raise CutExc()


_CUT2 = {"n": 0, "lim": 0}


def step2():
    _CUT2["n"] += 1
    if _CUT2["lim"] and _CUT2["n"] >= _CUT2["lim"]:
        raise CutExc()


def build(stop_after=None, taps=(), cut=0, cut2=0, opcut=0):
    _CUT["n"] = 0
    _CUT["lim"] = cut
    _CUT2["n"] = 0
    _CUT2["lim"] = cut2
    nc = bass.Bass("TRN2", target_bir_lowering=False)
    em = Em(nc)
    taps = list(taps)

    def ext_in(name, shape, dt=F32):
        return nc.dram_tensor(name, list(shape), dt, kind="ExternalInput")

    xin = ext_in("xin", [T, D])
    cvecs = ext_in("cvecs", [128, KC, 2])
    cmat = ext_in("cmat", [128, 256])
    rmask_in = ext_in("rmask", [128, T])
    ropeC = ext_in("ropeC", [64, T])
    ropeS = ext_in("ropeS", [64, T])
    selp = ext_in("selp", [128, 24])
    bmodF = ext_in("bmodF", [128, L * 96])
    bmodrow = ext_in("bmodrow", [2, L * 6 * D])
    gvec = ext_in("gvec", [128, L * 2 * KC + KC])
    gfin = ext_in("gfin", [1, D])
    w_kr2 = ext_in("w_kr2", [L, D, 128])
    w_g2 = ext_in("w_g2", [L, D, 32])
    wup = ext_in("wup", [L, 2, 16, 1024])
    bup = ext_in("bup", [128, L * 2 * 8])
    ggla = ext_in("ggla", [128, L * 4])
    gq = ext_in("gq", [128, L * 4])
    gkv = ext_in("gkv", [128, L * 4])
    yout = nc.dram_tensor("yout", [NLAT, D], F32, kind="ExternalOutput")

    def scr(name, shape, dt):
        return nc.dram_tensor(name, list(shape), dt)

    WSHAPE = {"w_mod": (D, 6 * D), "w_in": (D, NIN), "wq_n": (512, 2048), "wq_r": (512, 1024), "wq_s": (512, 1024),
              "wkv_n": (512, 2048), "wkv_v": (512, 2048), "w_a": (D, D), "w_b": (D, D), "w_o": (D, D),
              "w_fg": (D, DFF), "w_fu": (D, DFF), "w_fd": (DFF, D)}
    wfull = {}

    def Wt(name, l):
        key = (name, l)
        if key not in wfull:
            rows, cols = WSHAPE[name]
            sh = nc.dram_tensor(f"{name}_{l}_sh", [rows // 8, cols], F32, kind="ExternalInput")
            bn = nc.dram_tensor(f"{name}_{l}_bn", [rows // 8, cols], F32)
            full = nc.dram_tensor(f"{name}_{l}_full", [rows, cols], F32)
            b = Buf()
            em.dma("sp", bn.ap(), sh.ap(), writes=[b])
            em.coll(bn.ap().opt(), full.ap().opt(), reads=[b], writes=[b], group=[list(range(8))])
            wfull[key] = (full.ap(), b)
        return wfull[key]

    xres = scr("xres", [T, D], F32)
    yA = scr("yA", [8192, T], BF16)
    yB = scr("yB", [1184, T], F32)
    vtm = scr("vtm", [T, 2048], BF16)
    gaterow = scr("gaterow", [L, 2, 2, D], F32)
    qdT = [scr(f"qdT{d}", [1024, T], BF16) for d in range(2)]
    kdT = [scr(f"kdT{d}", [1024, T], BF16) for d in range(2)]
    kteT = [scr(f"kteT{d}", [1024, T], BF16) for d in range(2)]
    oT = [scr(f"oT{d}", [2048, T], F32) for d in range(2)]
    glaT = scr("glaT", [2048, T], BF16)
    mlaT = scr("mlaT", [2048, T], BF16)
    QnT = scr("QnT", [2048, T], BF16)
    QrT = scr("QrT", [1024, T], BF16)
    KnC = scr("KnC", [2048, NCTX], BF16)
    KrC = scr("KrC", [64, NCTX], BF16)
    VC = scr("VC", [NCTX, 2048], BF16)
    KnL = scr("KnL", [2048, NLAT], BF16)
    KrL = scr("KrL", [64, NLAT], BF16)
    VL = scr("VL", [NLAT, 2048], BF16)
    KnG = scr("KnG", [2 * 2048, NLAT], BF16)
    KrG = scr("KrG", [2 * 64, NLAT], BF16)
    VG = scr("VG", [2 * NLAT, 2048], BF16)
    sx_in = scr("sx_in", [128, 4096], F32)
    sx_out = scr("sx_out", [8 * 128, 4096], F32)
    KnG8 = scr("KnG8", [8 * 2048, NLAT], BF16)
    KrG8 = scr("KrG8", [8 * 64, NLAT], BF16)
    VG8 = scr("VG8", [8 * NLAT, 2048], BF16)

    tap_out = {}
    tap_src = {"xres": xres, "yA": yA, "yB": yB, "vtm": vtm, "gaterow": gaterow, "qdT0": qdT[0], "kdT0": kdT[0],
               "kteT0": kteT[0], "qdT1": qdT[1], "kdT1": kdT[1], "kteT1": kteT[1], "oT0": oT[0], "oT1": oT[1],
               "glaT": glaT, "mlaT": mlaT, "QnT": QnT, "QrT": QrT, "KnG": KnG, "KrG": KrG, "VG": VG, "KnC": KnC,
               "VC": VC, "KrC": KrC, "sx_out": sx_out, "KnL": KnL}
    for tname in taps:
        if tname in tap_src:
            h = tap_src[tname]
            tap_out[tname] = nc.dram_tensor("tap_" + tname, list(h.shape), h.dtype, kind="ExternalOutput")

    class Arena:
        def __init__(self):
            self.off = 16512
            self.n = 0

        def alloc(self, shape, dt, name=None):
            esz = 4 if dt == F32 else 2
            per = 1
            for s in shape[1:]:
                per *= s
            nb = (per * esz + 63) // 64 * 64
            self.n += 1
            t = nc.alloc_sbuf_tensor_at(f"{name or 'sb'}_{self.n}", list(shape), dt, offset=self.off)
            self.off += nb
            assert self.off <= 229344, f"SBUF overflow {self.off}"
            return t

        def mark(self):
            return self.off

        def release(self, m):
            self.off = m

    ar = Arena()
    psb = [nc.alloc_psum_tensor(f"psb{i}", [128, 512], F32) for i in range(8)]
    psbuf = [Buf() for _ in range(8)]
    pstate = {"i": 0}

    def ps_next():
        i = pstate["i"] % 8
        pstate["i"] += 1
        return psb[i], psbuf[i]

    ident_f = ar.alloc([128, 128], F32, "identf")
    ident = ar.alloc([128, 128], BF16, "ident")
    maskF = ar.alloc([64, 64], F32, "maskF")
    maskB = ar.alloc([64, 64], F32, "maskB")
    ones_f = ar.alloc([128, 128], F32, "ones")
    rmask = ar.alloc([128, T], F32, "rmask")
    sel = ar.alloc([128, 24], F32, "sel")
    bmodF_sb = ar.alloc([128, L * 96], F32, "bmodF")
    gvec_sb = ar.alloc([128, L * 2 * KC], F32, "gvec")
    bup_sb = ar.alloc([128, L * 16], F32, "bup")
    negb_sb = ar.alloc([128, L * 16], F32, "negb")
    ggla_sb = ar.alloc([128, L * 4], F32, "ggla")
    gq_sb = ar.alloc([128, L * 4], F32, "gq")
    gkv_sb = ar.alloc([128, L * 4], F32, "gkv")
    modF = ar.alloc([128, L * 4 * KC * 2], F32, "modF")
    gmod = ar.alloc([128, L * 2 * 2 * KC], F32, "gmod")
    scT = ar.alloc([128, KC, 2], BF16, "scT")
    cv_sb = ar.alloc([128, KC, 2], F32, "cv")
    dec_sb = ar.alloc([128, 2 * 8 * NCH], F32, "dec")
    cB = Buf()

    def modF_ap(l, seg, k, s):
        i = ((l * 4 + seg) * KC + k) * 2 + s
        return modF[:, i:i + 1]

    def gmod_ap(l, which, s, k):
        i = ((l * 2 + which) * 2 + s) * KC + k
        return gmod[:, i:i + 1]

    c0 = [
        em.dma("sp", ident_f[:], cmat.ap()[:, 0:128], writes=[cB]),
        em.dma("pool", ident[:], cmat.ap()[:, 0:128], writes=[cB]),
        em.dma("sp", maskF[:], cmat.ap()[0:64, 128:192], writes=[cB]),
        em.dma("sp", maskB[:], cmat.ap()[0:64, 192:256], writes=[cB]),
        em.dma("sp", rmask[:], rmask_in.ap()[:, :], writes=[cB]),
        em.dma("sp", sel[:], selp.ap()[:, :], writes=[cB]),
        em.dma("sp", bmodF_sb[:], bmodF.ap()[:, :], writes=[cB]),
        em.dma("sp", gvec_sb[:], gvec.ap()[:, 0:L * 2 * KC], writes=[cB]),
        em.dma("sp", bup_sb[:], bup.ap()[:, :], writes=[cB]),
        em.dma("sp", ggla_sb[:], ggla.ap()[:, :], writes=[cB]),
        em.dma("sp", gq_sb[:], gq.ap()[:, :], writes=[cB]),
        em.dma("sp", gkv_sb[:], gkv.ap()[:, :], writes=[cB]),
        em.dma("sp", cv_sb[:], cvecs.ap()[:, :, :], writes=[cB]),
    ]
    em.op("dve", lambda e: e.memset(ones_f[:], 1.0), writes=[cB])
    em.barrier()
    em.op("dve", lambda e: e.tensor_scalar(negb_sb[:], bup_sb[:], -1.0, None, ALU.mult), reads=[cB], writes=[cB])
    em.op("act", lambda e: e.activation(out=scT[:], in_=cv_sb[:], func=AF.Silu), reads=[cB], writes=[cB])
    em.barrier()

    def phase_end(name):
        em.barrier()
        return stop_after == name

    def finish():
        for tname, o in tap_out.items():
            em.dma("sp", o.ap(), tap_src[tname].ap())
        em.barrier(engines=["sp"])
        return nc

    if stop_after == "const":
        return finish(), tap_out

    def load_panel(dst, dstbuf, src_ap, nk, ncols, kgrp=4, rb=None):
        for k0 in range(0, nk, kgrp):
            k1 = min(nk, k0 + kgrp)
            em.dma("pool", dst[:, k0:k1, 0:ncols],
                   src_ap[k0 * 128:k1 * 128, :].rearrange("(k p) f -> p k f", p=128),
                   reads=([rb] if rb is not None else []), writes=[dstbuf])

    def norm_mod(l, which, xsrc, hxT, tiles):
        m = ar.mark()
        xt = [ar.alloc([128, D], F32, "xt") for _ in range(2)]
        xtb = [Buf() for _ in range(2)]
        junk = ar.alloc([128, D], BF16, "junk")
        junkb = Buf()
        xn = [ar.alloc([128, D], BF16, "xn") for _ in range(2)]
        xnb = [Buf() for _ in range(2)]
        st = [ar.alloc([128, 4], F32, "st") for _ in range(2)]
        stb = [Buf() for _ in range(2)]
        segs = 0 if which == 0 else 2
        hb = Buf()
        for it, tt in enumerate(tiles):
            s = 1 if tt < 2 else 0
            i = it % 2
            em.dma("sp", xt[i][:], xsrc.ap()[tt * 128:(tt + 1) * 128, :], writes=[xtb[i]])
            em.op("act", lambda e, i=i: e.activation(out=junk[:], in_=xt[i][:], func=AF.Square, accum_out=st[i][:, 0:1]),
                  reads=[xtb[i]], writes=[junkb, stb[i]])
            em.op("act", lambda e, i=i: e.activation(out=st[i][:, 1:2], in_=st[i][:, 0:1], func=AF.Sqrt, scale=1.0 / D, bias=eps_sb[:, 0:1]),
                  reads=[stb[i]], writes=[stb[i]])
            em.op("dve", lambda e, i=i: e.reciprocal(st[i][:, 2:3], st[i][:, 1:2]), reads=[stb[i]], writes=[stb[i]])
            em.op("dve", lambda e, i=i: e.tensor_scalar(xn[i][:], xt[i][:], st[i][:, 2:3], None, ALU.mult),
                  reads=[xtb[i], stb[i]], writes=[xnb[i]])
            for half in range(2):
                pt, pb = ps_next()
                ptb = pt[:].bitcast(BF16)
                em.tr([(ptb[:, j * 128:(j + 1) * 128], xn[i][:, (half * 8 + j) * 128:(half * 8 + j + 1) * 128]) for j in range(8)],
                      [xnb[i], cB], pb, ident[:])
                for j in range(8):
                    k = half * 8 + j
                    if j % 2 == 0:
                        em.op("act", lambda e, j=j, k=k, s=s, tt=tt, ptb=ptb: e.activation(
                            out=hxT[:, k, tt * 128:(tt + 1) * 128], in_=ptb[:, j * 128:(j + 1) * 128], func=AF.Identity,
                            scale=gmod_ap(l, which, s, k), bias=modF_ap(l, segs, k, s)), reads=[pb, cB], writes=[hb])
                    else:
                        em.op("dve", lambda e, j=j, k=k, s=s, tt=tt, ptb=ptb: e.tensor_scalar(
                            hxT[:, k, tt * 128:(tt + 1) * 128], ptb[:, j * 128:(j + 1) * 128],
                            gmod_ap(l, which, s, k), modF_ap(l, segs, k, s), ALU.mult, ALU.add), reads=[pb, cB], writes=[hb])
        ar.release(m)
        return hb

    eps_sb = ar.alloc([128, 2], F32, "eps")
    em.op("dve", lambda e: e.memset(eps_sb[:, 0:1], EPS), writes=[cB])
    em.op("dve", lambda e: e.memset(eps_sb[:, 1:2], 1.0), writes=[cB])
    em.barrier()

    def gemm_fm(actT, actb, nk, wsrc, ncols, chunks, epilogue, wslots, wbufs, pw=512):
        pidx = 0
        wrb = None
        if isinstance(wsrc, tuple):
            wsrc, wrb = wsrc
        for c0_ in range(0, ncols, pw):
            cw = min(pw, ncols - c0_)
            w = wslots[pidx % len(wslots)]
            wb = wbufs[pidx % len(wslots)]
            pidx += 1
            load_panel(w, wb, wsrc[:, c0_:c0_ + cw], nk, cw, rb=wrb)
            for f0 in range(0, cw, 128):
                nf = min(128, cw - f0)
                for (t0, tn) in chunks:
                    pt, pb = ps_next()
                    em.mm(pt[0:nf, 0:tn], [(w[:, k, f0:f0 + nf], actT[:, k, t0:t0 + tn]) for k in range(nk)], [wb, actb], pb)
                    epilogue(c0_ + f0, nf, t0, tn, pt, pb)

    xcur = xin
    for l in range(L):
        last = l == L - 1
        q_t0 = NCTX if last else 0
        out_tiles = list(range(q_t0 // 128, T // 128))
        out_chunks = tok_chunks(q_t0, T)
        all_chunks = tok_chunks(0, T)

        m0 = ar.mark()
        wp = [ar.alloc([128, KC, 512], BF16, "wp") for _ in range(3)]
        wpb = [Buf() for _ in range(3)]
        rowst = [ar.alloc([2, 512], F32, "rowst") for _ in range(2)]
        rowb = [Buf() for _ in range(2)]
        brow = ar.alloc([2, 512], F32, "brow")
        browb = Buf()
        pi = 0
        wmod_ap, wmod_b = Wt("w_mod", l)
        if True:
            for p in range(24):
                seg = p // 4
                w = wp[pi % 3]
                wb = wpb[pi % 3]
                load_panel(w, wb, wmod_ap[:, p * 512:(p + 1) * 512], KC, 512, rb=wmod_b)
                if seg in (0, 1, 3, 4):
                    s4 = {0: 0, 1: 1, 3: 2, 4: 3}[seg]
                    pt, pb = ps_next()
                    for j in range(4):
                        em.mm(pt[:, j * 2:(j + 1) * 2], [(w[:, k, j * 128:(j + 1) * 128], scT[:, k, :]) for k in range(KC)],
                              [wb, cB], pb)
                    for j in range(4):
                        kf = (p % 4) * 4 + j
                        col = l * 96 + seg * 16 + kf
                        i0 = ((l * 4 + s4) * KC + kf) * 2
                        em.op("dve", lambda e, j=j, col=col, i0=i0: e.tensor_scalar(
                            modF[:, i0:i0 + 2], pt[:, j * 2:(j + 1) * 2], bmodF_sb[:, col:col + 1], None, ALU.add),
                            reads=[pb, cB], writes=[cB])
                else:
                    g2 = 0 if seg == 2 else 1
                    pt, pb = ps_next()
                    em.mm(pt[0:2, :], [(scT[:, k, :], w[:, k, :]) for k in range(KC)], [wb, cB], pb)
                    em.dma("sp", brow[:], bmodrow.ap()[:, l * 6 * D + p * 512:l * 6 * D + (p + 1) * 512], writes=[browb])
                    rs = rowst[pi % 2]
                    rb = rowb[pi % 2]
                    em.op("dve", lambda e, rs=rs: e.tensor_tensor(rs[:], pt[0:2, :], brow[:], ALU.add),
                          reads=[pb, browb], writes=[rb])
                    c0_ = (p % 4) * 512
                    em.dma("sp", gaterow.ap()[l, g2, :, c0_:c0_ + 512], rs[:], reads=[rb])
                pi += 1
        em.barrier()
        if True:
            for which in range(2):
                segc = 1 if which == 0 else 3
                for s in range(2):
                    for k in range(KC):
                        gi = (l * 2 + which) * KC + k
                        em.op("dve", lambda e, l=l, which=which, s=s, k=k, gi=gi, segc=segc: e.scalar_tensor_tensor(
                            gmod_ap(l, which, s, k), modF_ap(l, segc, k, s), 1.0, gvec_sb[:, gi:gi + 1], ALU.add, ALU.mult),
                            reads=[cB], writes=[cB])
        ar.release(m0)
        if phase_end(f"mod{l}"):
            return finish(), tap_out


        mA = ar.mark()
        hxT = ar.alloc([128, KC, T], BF16, "hxT")
        hb = norm_mod(l, 0, xcur, hxT, list(range(T // 128)))
        em.barrier()
        wsl = [ar.alloc([128, KC, 512], BF16, "wsl") for _ in range(3)]
        wslb = [Buf() for _ in range(3)]
        stg = [ar.alloc([128, T], BF16, "stg") for _ in range(3)]
        stgb = [Buf() for _ in range(3)]
        stgf = [ar.alloc([128, T], F32, "stgf") for _ in range(2)]
        stgfb = [Buf() for _ in range(2)]
        cnt = {"a": 0, "b": 0, "e": 0}

        def mk_epi_A(kind, dst_row0):
            def epi(f0, nf, t0, tn, pt, pb):
                i = cnt["a"] % 3
                s, sb_ = stg[i], stgb[i]
                if kind == "q":
                    em.op("act", lambda e: e.activation(out=s[0:nf, t0:t0 + tn], in_=pt[0:nf, 0:tn], func=AF.Copy, scale=1.0 / 16.0),
                          reads=[pb], writes=[sb_])
                elif kind == "k":
                    if cnt["e"] % 2 == 0:
                        em.op("dve", lambda e: e.tensor_copy(s[0:nf, t0:t0 + tn], pt[0:nf, 0:tn]), reads=[pb], writes=[sb_])
                    else:
                        em.op("act", lambda e: e.activation(out=s[0:nf, t0:t0 + tn], in_=pt[0:nf, 0:tn], func=AF.Copy),
                              reads=[pb], writes=[sb_])
                    cnt["e"] += 1
                elif kind == "silu":
                    em.op("act", lambda e: e.activation(out=s[0:nf, t0:t0 + tn], in_=pt[0:nf, 0:tn], func=AF.Silu),
                          reads=[pb], writes=[sb_])
                else:
                    em.op("act", lambda e: e.activation(out=s[0:nf, t0:t0 + tn], in_=pt[0:nf, 0:tn], func=AF.Sigmoid),
                          reads=[pb], writes=[sb_])
                if t0 + tn == T:
                    r0 = dst_row0 + f0
                    em.dma("sp", yA.ap()[r0:r0 + nf, :], s[0:nf, :], reads=[sb_])
                    cnt["a"] += 1
            return epi

        def mk_epi_B(dst_row0):
            def epi(f0, nf, t0, tn, pt, pb):
                i = cnt["b"] % 2
                s, sb_ = stgf[i], stgfb[i]
                em.op("dve", lambda e: e.tensor_copy(s[0:nf, t0:t0 + tn], pt[0:nf, 0:tn]), reads=[pb], writes=[sb_])
                if t0 + tn == T:
                    r0 = dst_row0 + f0
                    em.dma("sp", yB.ap()[r0:r0 + nf, :], s[0:nf, :], reads=[sb_])
                    cnt["b"] += 1
            return epi

        wl, wlb = Wt("w_in", l)
        gemm_fm(hxT, hb, KC, (wl[:, OFF_Q:OFF_Q + 1024], wlb), 1024, all_chunks, mk_epi_A("q", 0), wsl, wslb)
        gemm_fm(hxT, hb, KC, (wl[:, OFF_K:OFF_K + 1024], wlb), 1024, all_chunks, mk_epi_A("k", 1024), wsl, wslb)
        gemm_fm(hxT, hb, KC, (wl[:, OFF_R:OFF_R + 2048], wlb), 2048, all_chunks, mk_epi_A("silu", 2048), wsl, wslb)
        gemm_fm(hxT, hb, KC, (wl[:, OFF_BGA:OFF_BGA + 2048], wlb), 2048, all_chunks, mk_epi_A("sig", 4096), wsl, wslb)
        gemm_fm(hxT, hb, KC, (wl[:, OFF_BGB:OFF_BGB + 2048], wlb), 2048, all_chunks, mk_epi_A("sig", 6144), wsl, wslb)
        gemm_fm(hxT, hb, KC, (wl[:, OFF_CQ:OFF_CQ + 1024], wlb), 1024, all_chunks, mk_epi_B(0), wsl, wslb)
        gemm_fm(hxT, hb, KC, w_kr2.ap()[l], 128, all_chunks, mk_epi_B(1024), wsl, wslb)
        gemm_fm(hxT, hb, KC, w_g2.ap()[l], 32, all_chunks, mk_epi_B(1152), wsl, wslb)
        vst = [ar.alloc([128, 512], BF16, "vst") for _ in range(3)]
        vstb = [Buf() for _ in range(3)]
        vi = 0
        for fc in range(4):
            w = wsl[fc % 3]
            wb = wslb[fc % 3]
            load_panel(w, wb, wl[:, OFF_V + fc * 512:OFF_V + (fc + 1) * 512], KC, 512, rb=wlb)
            for tt in range(T // 128):
                pt, pb = ps_next()
                em.mm(pt[:, :], [(hxT[:, k, tt * 128:(tt + 1) * 128], w[:, k, :]) for k in range(KC)], [wb, hb], pb)
                s, sb_ = vst[vi % 3], vstb[vi % 3]
                if vi % 2 == 0:
                    em.op("dve", lambda e, s=s, pt=pt: e.tensor_copy(s[:], pt[:, :]), reads=[pb], writes=[sb_])
                else:
                    em.op("act", lambda e, s=s, pt=pt: e.activation(out=s[:], in_=pt[:, :], func=AF.Copy), reads=[pb], writes=[sb_])
                em.dma("sp", vtm.ap()[tt * 128:(tt + 1) * 128, fc * 512:(fc + 1) * 512], s[:], reads=[sb_])
                vi += 1
        ar.release(mA)
        if phase_end(f"inproj{l}"):
            return finish(), tap_out

        try:
            mB = ar.mark()
            g16f = ar.alloc([16, 2, T], F32, "g16f")
            g16 = ar.alloc([16, 2, T], BF16, "g16")
            wupb = ar.alloc([16, 2, 1024], BF16, "wupb")
            gB_ = Buf()
            em.dma("sp", g16f[:, 0, :], yB.ap()[1152:1168, :], writes=[gB_])
            em.dma("sp", g16f[:, 1, :], yB.ap()[1168:1184, :], writes=[gB_])
            em.dma("pool", wupb[:, 0, :], wup.ap()[l, 0], writes=[gB_])
            em.dma("pool", wupb[:, 1, :], wup.ap()[l, 1], writes=[gB_])
            em.op("dve", lambda e: e.tensor_copy(g16[:], g16f[:]), reads=[gB_], writes=[gB_])
            e1 = ar.alloc([128, T], F32, "e1")
            sp_ = ar.alloc([128, T], F32, "sp")
            cs = ar.alloc([128, T], F32, "cs")
            G = ar.alloc([128, T], F32, "G")
            tmp = ar.alloc([128, T], F32, "tmp")
            Eq = ar.alloc([128, T], F32, "Eq")
            Ek = ar.alloc([128, T], F32, "Ek")
            Ete = ar.alloc([128, T], F32, "Ete")
            qk = [ar.alloc([128, 2, T], BF16, "qk") for _ in range(2)]
            qkb = [Buf() for _ in range(2)]
            ost = [ar.alloc([128, 3, T], BF16, "ost") for _ in range(2)]
            ostb = [Buf() for _ in range(2)]
            wB = Buf()
            it = 0
            for d in range(2):
                for p in range(8):
                    i = it % 2
                    it += 1
                    em.dma("sp", qk[i][:, 0, :], yA.ap()[p * 128:(p + 1) * 128, :], writes=[qkb[i]])
                    em.dma("sp", qk[i][:, 1, :], yA.ap()[1024 + p * 128:1024 + (p + 1) * 128, :], writes=[qkb[i]])
                    bcol = (l * 2 + d) * 8 + p
                    for (t0, tn) in all_chunks:
                        pt, pb = ps_next()
                        em.mm(pt[:, 0:tn], [(wupb[:, d, p * 128:(p + 1) * 128], g16[:, d, t0:t0 + tn])], [gB_], pb)
                        em.op("act", lambda e, pt=pt, t0=t0, tn=tn, bcol=bcol: e.activation(
                            out=e1[:, t0:t0 + tn], in_=pt[:, 0:tn], func=AF.Exp, scale=-1.0, bias=negb_sb[:, bcol:bcol + 1]),
                            reads=[pb, cB], writes=[wB])
                    step()
                    em.op("act", lambda e: e.activation(out=sp_[:], in_=e1[:], func=AF.Ln, bias=eps_sb[:, 1:2]), reads=[wB], writes=[wB])
                    step()
                    em.op("dve", lambda e: e.tensor_tensor_scan(cs[:], rmask[:], sp_[:], 0.0, ALU.mult, ALU.add), reads=[wB, cB], writes=[wB])
                    step()
                    cs3 = cs[:].rearrange("p (c j) -> p c j", j=CH)
                    tot_bc = cs3[:, :, CH - 1:CH].broadcast_to([128, NCH, CH])
                    G3 = G[:].rearrange("p (c j) -> p c j", j=CH)
                    tmp3 = tmp[:].rearrange("p (c j) -> p c j", j=CH)
                    if d == 0:
                        Gs = cs
                        Gs3 = cs3
                    else:
                        em.op("dve", lambda e: e.tensor_tensor(tmp[:], sp_[:], cs[:], ALU.subtract), reads=[wB], writes=[wB])
                        em.op("dve", lambda e: e.tensor_tensor(G3, tmp3, tot_bc, ALU.add), reads=[wB], writes=[wB])
                        Gs = G
                        Gs3 = G3
                    step()
                    em.op("act", lambda e, Gs=Gs: e.activation(out=Eq[:], in_=Gs[:], func=AF.Exp, scale=-1.0 / 16.0), reads=[wB], writes=[wB])
                    em.op("act", lambda e, Gs=Gs: e.activation(out=Ek[:], in_=Gs[:], func=AF.Exp, scale=1.0 / 16.0), reads=[wB], writes=[wB])
                    step()
                    em.op("dve", lambda e, Gs3=Gs3: e.tensor_tensor(tmp3, Gs3, tot_bc, ALU.subtract), reads=[wB], writes=[wB])
                    em.op("act", lambda e: e.activation(out=Ete[:], in_=tmp[:], func=AF.Exp, scale=1.0 / 16.0), reads=[wB], writes=[wB])
                    step()
                    dcol = (d * 8 + p) * NCH
                    em.op("act", lambda e, dcol=dcol: e.activation(
                        out=dec_sb[:, dcol:dcol + NCH].rearrange("p (c o) -> p c o", o=1), in_=cs3[:, :, CH - 1:CH], func=AF.Exp, scale=-1.0 / 16.0),
                        reads=[wB], writes=[cB])
                    step()
                    o_ = ost[i]
                    em.op("dve", lambda e, o_=o_, i=i: e.tensor_tensor(o_[:, 0, :], qk[i][:, 0, :], Eq[:], ALU.mult), reads=[wB, qkb[i]], writes=[ostb[i]])
                    em.op("dve", lambda e, o_=o_, i=i: e.tensor_tensor(o_[:, 1, :], qk[i][:, 1, :], Ek[:], ALU.mult), reads=[wB, qkb[i]], writes=[ostb[i]])
                    em.op("dve", lambda e, o_=o_, i=i: e.tensor_tensor(o_[:, 2, :], qk[i][:, 1, :], Ete[:], ALU.mult), reads=[wB, qkb[i]], writes=[ostb[i]])
                    step()
                    em.dma("sp", qdT[d].ap()[p * 128:(p + 1) * 128, :], o_[:, 0, :], reads=[ostb[i]])
                    em.dma("sp", kdT[d].ap()[p * 128:(p + 1) * 128, :], o_[:, 1, :], reads=[ostb[i]])
                    em.dma("sp", kteT[d].ap()[p * 128:(p + 1) * 128, :], o_[:, 2, :], reads=[ostb[i]])
        except CutExc:
            em.barrier()
            return finish(), tap_out
        ar.release(mB)
        if phase_end(f"glaprep{l}"):
            return finish(), tap_out

        try:
            mC = ar.mark()
            S = ar.alloc([128, 2, 4, 512], F32, "S")
            Sb = ar.alloc([128, 2, 4, 512], BF16, "Sb")
            SB_ = Buf()
            SbB = [[Buf() for _ in range(2)] for _ in range(4)]
            GRP = 256
            qg = [ar.alloc([128, 8, GRP], BF16, "qg") for _ in range(2)]
            kg = [ar.alloc([128, 8, GRP], BF16, "kg") for _ in range(2)]
            teg = [ar.alloc([128, 8, GRP], BF16, "teg") for _ in range(2)]
            vg = [ar.alloc([64, 4, 2048], BF16, "vg") for _ in range(2)]
            ldb = [Buf() for _ in range(2)]
            ostg = [ar.alloc([128, 16, GRP], F32, "ostg") for _ in range(2)]
            ostgb = [Buf() for _ in range(2)]
            attm = [ar.alloc([64, 64], BF16, "attm") for _ in range(4)]
            attmb = [Buf() for _ in range(4)]
            ktm = [ar.alloc([64, 256], BF16, "ktm") for _ in range(4)]
            ktmb = [Buf() for _ in range(4)]
            Gr = [ar.alloc([128, 4096], F32, "Gr") for _ in range(2)]
            Grb = [Buf() for _ in range(2)]
            gxb = Buf()
            Sflat = S[:].rearrange("p a h v -> p (a h v)")
            Sbflat = Sb[:].rearrange("p a h v -> p (a h v)")

            def zero_state():
                em.op("dve", lambda e: e.memset(Sflat, 0.0), writes=[SB_])
                allsb = [b for hb_ in SbB for b in hb_]
                em.op("dve", lambda e: e.memset(Sbflat, 0.0), writes=allsb)

            ctr = {"g": 0, "a": 0}

            def scan_groups(d, groups, descending, want_out, want_state=True):
                mask = maskF if (d == 0) else maskB
                for g in groups:
                    gi = ctr["g"] % 2
                    ctr["g"] += 1
                    t0 = g * GRP
                    for p in range(8):
                        pass
                    em.dma("sp", qg[gi][:], qdT[d].ap()[:, t0:t0 + GRP].rearrange("(k p) t -> p k t", p=128), writes=[ldb[gi]])
                    em.dma("sp", kg[gi][:], kdT[d].ap()[:, t0:t0 + GRP].rearrange("(k p) t -> p k t", p=128), writes=[ldb[gi]])
                    em.dma("sp", teg[gi][:], kteT[d].ap()[:, t0:t0 + GRP].rearrange("(k p) t -> p k t", p=128), writes=[ldb[gi]])
                    em.dma("sp", vg[gi][:], vtm.ap()[t0:t0 + GRP, :].rearrange("(c j) f -> j c f", j=CH), writes=[ldb[gi]])
                    corder = list(range(4))
                    if descending:
                        corder = corder[::-1]
                    for c in corder:
                        chunk = (t0 // CH) + c
                        cs_ = slice(c * CH, (c + 1) * CH)
                        for h in range(4):
                            step2()
                            ai = ctr["a"] % 4
                            ctr["a"] += 1
                            if want_out:
                                pa, pab = ps_next()
                                em.mm(pa[0:64, 0:64], [(kg[gi][:, h * 2 + a, cs_], qg[gi][:, h * 2 + a, cs_]) for a in range(2)], [ldb[gi]], pab)
                                em.op("dve", lambda e, ai=ai, pa=pa, mask=mask: e.tensor_tensor(attm[ai][:], pa[0:64, 0:64], mask[:], ALU.mult),
                                      reads=[pab, cB], writes=[attmb[ai]])
                                po, pob = ps_next()
                                for dvc in range(4):
                                    pairs = [(vg[gi][:, c, h * 512 + dvc * 128:h * 512 + (dvc + 1) * 128], attm[ai][:])]
                                    pairs += [(Sb[:, a, h, dvc * 128:(dvc + 1) * 128], qg[gi][:, h * 2 + a, cs_]) for a in range(2)]
                                    em.mm(po[:, dvc * 64:(dvc + 1) * 64], pairs, [ldb[gi], attmb[ai], SbB[h][0], SbB[h][1]], pob)
                                og = ostg[gi]
                                src = po[:, 0:256].rearrange("p (v j) -> p v j", j=64)
                                dst = og[:, h * 4:(h + 1) * 4, cs_]
                                if h % 2 == 0:
                                    em.op("act", lambda e, src=src, dst=dst: e.activation(out=dst, in_=src, func=AF.Copy), reads=[pob], writes=[ostgb[gi]])
                                else:
                                    em.op("dve", lambda e, src=src, dst=dst: e.tensor_copy(dst, src), reads=[pob], writes=[ostgb[gi]])
                            if want_state:
                                pk, pkb = ps_next()
                                pkb16 = pk[:].bitcast(BF16)
                                em.tr([(pkb16[0:64, a * 128:(a + 1) * 128], teg[gi][:, h * 2 + a, cs_]) for a in range(2)],
                                      [ldb[gi], cB], pkb, ident[:])
                                em.op("act", lambda e, ai=ai, pkb16=pkb16: e.activation(out=ktm[ai][:], in_=pkb16[0:64, 0:256], func=AF.Copy),
                                      reads=[pkb], writes=[ktmb[ai]])
                                for a in range(2):
                                    pS, pSb = ps_next()
                                    em.mm(pS[:, :], [(ktm[ai][:, a * 128:(a + 1) * 128], vg[gi][:, c, h * 512:(h + 1) * 512])], [ktmb[ai], ldb[gi]], pSb)
                                    dcol = (d * 8 + h * 2 + a) * NCH + chunk
                                    em.op("dve", lambda e, a=a, h=h, pS=pS, dcol=dcol: e.scalar_tensor_tensor(
                                        S[:, a, h, :], S[:, a, h, :], dec_sb[:, dcol:dcol + 1], pS[:, :], ALU.mult, ALU.add),
                                        reads=[pSb, SB_, cB], writes=[SB_])
                                    em.op("act", lambda e, a=a, h=h: e.activation(out=Sb[:, a, h, :], in_=S[:, a, h, :], func=AF.Copy),
                                          reads=[SB_], writes=[SbB[h][a]])
                    if want_out:
                        em.dma("sp", oT[d].ap()[:, t0:t0 + GRP].rearrange("(r p) t -> p r t", p=128), ostg[gi][:], reads=[ostgb[gi]])

            zero_state()
            scan_groups(0, [0], False, want_out=not last)
            scan_groups(0, [1, 2, 3, 4], False, want_out=True)
            step2()
            em.dma("sp", sx_in.ap(), Sflat, reads=[SB_])
            em.barrier(engines=["pool"])
            em.coll(sx_in.ap().opt(), sx_out.ap().opt(), writes=[gxb])
            step2()
            step2()
            if not last:
                zero_state()
                scan_groups(1, [0], True, want_out=True, want_state=True)
            step2()
            allsb = [b for hb_ in SbB for b in hb_]
            for r8 in range(8):
                gi8 = r8 % 2
                em.dma("sp", Gr[gi8][:], sx_out.ap()[r8 * 128:(r8 + 1) * 128, :], reads=[gxb], writes=[Grb[gi8]])
                if r8 == 0:
                    em.op("dve", lambda e, gi8=gi8, r8=r8: e.tensor_scalar(Sflat, Gr[gi8][:], sel[:, r8:r8 + 1], None, ALU.mult),
                          reads=[Grb[gi8], cB], writes=[SB_])
                else:
                    em.op("dve", lambda e, gi8=gi8, r8=r8: e.scalar_tensor_tensor(Sflat, Gr[gi8][:], sel[:, r8:r8 + 1], Sflat, ALU.mult, ALU.add),
                          reads=[Grb[gi8], cB, SB_], writes=[SB_])
            em.op("act", lambda e: e.activation(out=Sbflat, in_=Sflat, func=AF.Copy), reads=[SB_], writes=allsb)
            scan_groups(1, [4, 3, 2, 1], True, want_out=True)
        except CutExc:
            em.barrier()
            return finish(), tap_out
        ar.release(mC)
        if phase_end(f"glascan{l}"):
            return finish(), tap_out

        mD = ar.mark()
        of = [ar.alloc([128, 4, 512], F32, "of") for _ in range(2)]
        ob = [ar.alloc([128, 4, 512], F32, "ob") for _ in range(2)]
        rr = [ar.alloc([128, 4, 512], BF16, "rr") for _ in range(2)]
        inb = [Buf() for _ in range(2)]
        sq = ar.alloc([128, 4, 512], F32, "sq")
        rstd = ar.alloc([128, 512], F32, "rstd")
        wkb = Buf()
        gst = [ar.alloc([128, 4, 512], BF16, "gst") for _ in range(2)]
        gstb = [Buf() for _ in range(2)]
        it = 0
        for h in range(4):
            for (t0, tn) in out_chunks:
                i = it % 2
                it += 1
                rows = slice(h * 512, (h + 1) * 512)
                em.dma("sp", of[i][:, :, 0:tn], oT[0].ap()[rows, t0:t0 + tn].rearrange("(r p) t -> p r t", p=128), writes=[inb[i]])
                em.dma("sp", ob[i][:, :, 0:tn], oT[1].ap()[rows, t0:t0 + tn].rearrange("(r p) t -> p r t", p=128), writes=[inb[i]])
                em.dma("sp", rr[i][:, :, 0:tn], yA.ap()[2048 + h * 512:2048 + (h + 1) * 512, t0:t0 + tn].rearrange("(r p) t -> p r t", p=128), writes=[inb[i]])
                em.op("dve", lambda e, i=i, tn=tn: e.tensor_tensor(of[i][:, :, 0:tn], of[i][:, :, 0:tn], ob[i][:, :, 0:tn], ALU.add), reads=[inb[i]], writes=[inb[i]])
                em.op("act", lambda e, i=i, tn=tn: e.activation(out=sq[:, :, 0:tn], in_=of[i][:, :, 0:tn], func=AF.Square), reads=[inb[i]], writes=[wkb])
                pt, pb = ps_next()
                em.mm(pt[:, 0:tn], [(ones_f[:], sq[:, r, 0:tn]) for r in range(4)], [wkb, cB], pb)
                em.op("act", lambda e, pt=pt, tn=tn: e.activation(out=rstd[:, 0:tn], in_=pt[:, 0:tn], func=AF.Sqrt, scale=1.0 / 512.0, bias=eps_sb[:, 0:1]),
                      reads=[pb, cB], writes=[wkb])
                em.op("dve", lambda e, tn=tn: e.reciprocal(rstd[:, 0:tn], rstd[:, 0:tn]), reads=[wkb], writes=[wkb])
                for r in range(4):
                    em.op("dve", lambda e, i=i, r=r, tn=tn: e.tensor_tensor(of[i][:, r, 0:tn], of[i][:, r, 0:tn], rstd[:, 0:tn], ALU.mult),
                          reads=[inb[i], wkb], writes=[inb[i]])
                    gcol = l * 4 + r
                    em.op("dve", lambda e, i=i, r=r, tn=tn, gcol=gcol: e.scalar_tensor_tensor(
                        gst[i][:, r, 0:tn], of[i][:, r, 0:tn], ggla_sb[:, gcol:gcol + 1], rr[i][:, r, 0:tn], ALU.mult, ALU.mult),
                        reads=[inb[i], cB], writes=[gstb[i]])
                em.dma("sp", glaT.ap()[rows, t0:t0 + tn].rearrange("(r p) t -> p r t", p=128), gst[i][:, :, 0:tn], reads=[gstb[i]])
        ar.release(mD)
        if phase_end(f"glaout{l}"):
            return finish(), tap_out

        mE = ar.mark()
        cqn = ar.alloc([128, 4, T], BF16, "cqn")
        ckvn = ar.alloc([128, 4, T], BF16, "ckvn")
        nB = Buf()
        cin = [ar.alloc([128, 4, 512], F32, "cin") for _ in range(2)]
        cinb = [Buf() for _ in range(2)]
        sq2 = ar.alloc([128, 4, 512], F32, "sq2")
        rs2 = ar.alloc([128, 512], F32, "rs2")
        wk2 = Buf()
        it = 0
        for which, (dstn, gsb) in enumerate(((cqn, gq_sb), (ckvn, gkv_sb))):
            for (t0, tn) in all_chunks:
                i = it % 2
                it += 1
                em.dma("sp", cin[i][:, :, 0:tn], yB.ap()[which * 512:(which + 1) * 512, t0:t0 + tn].rearrange("(r p) t -> p r t", p=128), writes=[cinb[i]])
                em.op("act", lambda e, i=i, tn=tn: e.activation(out=sq2[:, :, 0:tn], in_=cin[i][:, :, 0:tn], func=AF.Square), reads=[cinb[i]], writes=[wk2])
                pt, pb = ps_next()
                em.mm(pt[:, 0:tn], [(ones_f[:], sq2[:, r, 0:tn]) for r in range(4)], [wk2, cB], pb)
                em.op("act", lambda e, pt=pt, tn=tn: e.activation(out=rs2[:, 0:tn], in_=pt[:, 0:tn], func=AF.Sqrt, scale=1.0 / 512.0, bias=eps_sb[:, 0:1]),
                      reads=[pb, cB], writes=[wk2])
                em.op("dve", lambda e, tn=tn: e.reciprocal(rs2[:, 0:tn], rs2[:, 0:tn]), reads=[wk2], writes=[wk2])
                for r in range(4):
                    gcol = l * 4 + r
                    em.op("dve", lambda e, i=i, r=r, tn=tn, t0=t0, gcol=gcol, dstn=dstn, gsb=gsb: e.scalar_tensor_tensor(
                        dstn[:, r, t0:t0 + tn], cin[i][:, r, 0:tn], gsb[:, gcol:gcol + 1], rs2[:, 0:tn], ALU.mult, ALU.mult),
                        reads=[cinb[i], wk2, cB], writes=[nB])
        em.barrier()
        rc = ar.alloc([64, T], F32, "rc")
        rsn = ar.alloc([64, T], F32, "rsn")
        krf = ar.alloc([64, 2, T], F32, "krf")
        krt = ar.alloc([64, T], F32, "krt")
        kro = ar.alloc([64, T], BF16, "kro")
        rB = Buf()
        em.dma("sp", rc[:], ropeC.ap(), writes=[rB])
        em.dma("sp", rsn[:], ropeS.ap(), writes=[rB])
        em.dma("sp", krf[:, 0, :], yB.ap()[1024:1088, :], writes=[rB])
        em.dma("sp", krf[:, 1, :], yB.ap()[1088:1152, :], writes=[rB])
        em.op("dve", lambda e: e.tensor_tensor(krt[:], krf[:, 0, :], rc[:], ALU.mult), reads=[rB], writes=[rB])
        em.op("dve", lambda e: e.tensor_tensor(krf[:, 1, :], krf[:, 1, :], rsn[:], ALU.mult), reads=[rB], writes=[rB])
        em.op("dve", lambda e: e.tensor_tensor(kro[:], krt[:], krf[:, 1, :], ALU.add), reads=[rB], writes=[rB])
        em.dma("sp", KrC.ap(), kro[:, 0:NCTX], reads=[rB])
        em.dma("sp", KrL.ap(), kro[:, NCTX:T], reads=[rB])
        wsl = [ar.alloc([128, 4, 512], BF16, "wslE") for _ in range(3)]
        wslb = [Buf() for _ in range(3)]
        stg = [ar.alloc([128, T], BF16, "stgE") for _ in range(3)]
        stgb = [Buf() for _ in range(3)]
        cnt = {"a": 0, "e": 0}
        q_chunks = out_chunks

        def mk_epi_E(dsts):
            def epi(f0, nf, t0, tn, pt, pb):
                i = cnt["a"] % 3
                s, sb_ = stg[i], stgb[i]
                if cnt["e"] % 2 == 0:
                    em.op("dve", lambda e: e.tensor_copy(s[0:nf, t0:t0 + tn], pt[0:nf, 0:tn]), reads=[pb], writes=[sb_])
                else:
                    em.op("act", lambda e: e.activation(out=s[0:nf, t0:t0 + tn], in_=pt[0:nf, 0:tn], func=AF.Copy), reads=[pb], writes=[sb_])
                cnt["e"] += 1
                if t0 + tn == T:
                    for (dram, c_lo, c_hi, dcol0) in dsts:
                        lo = max(c_lo, q_t0 if dram is QnT else c_lo)
                        em.dma("sp", dram.ap()[f0:f0 + nf, lo - dcol0:c_hi - dcol0], s[0:nf, lo:c_hi], reads=[sb_])
                    cnt["a"] += 1
            return epi

        gemm_fm(cqn, nB, 4, Wt("wq_n", l), 2048, q_chunks, mk_epi_E([(QnT, 0, T, 0)]), wsl, wslb)
        gemm_fm(ckvn, nB, 4, Wt("wkv_n", l), 2048, all_chunks, mk_epi_E([(KnC, 0, NCTX, 0), (KnL, NCTX, T, NCTX)]), wsl, wslb)
        wr = ar.alloc([128, 4, 1024], BF16, "wr")
        ws = ar.alloc([128, 4, 1024], BF16, "ws")
        wrB = Buf()
        load_panel(wr, wrB, Wt("wq_r", l)[0], 4, 1024, rb=Wt("wq_r", l)[1])
        load_panel(ws, wrB, Wt("wq_s", l)[0], 4, 1024, rb=Wt("wq_s", l)[1])
        qrs = [ar.alloc([64, T], BF16, "qrs") for _ in range(2)]
        qrsb = [Buf() for _ in range(2)]
        t1_ = ar.alloc([64, 512], F32, "t1")
        t2_ = ar.alloc([64, 512], F32, "t2")
        tB = Buf()
        for h in range(16):
            i = h % 2
            for (t0, tn) in q_chunks:
                p1, p1b = ps_next()
                em.mm(p1[0:64, 0:tn], [(wr[:, k, h * 64:(h + 1) * 64], cqn[:, k, t0:t0 + tn]) for k in range(4)], [wrB, nB], p1b)
                p2, p2b = ps_next()
                em.mm(p2[0:64, 0:tn], [(ws[:, k, h * 64:(h + 1) * 64], cqn[:, k, t0:t0 + tn]) for k in range(4)], [wrB, nB], p2b)
                em.op("dve", lambda e, p1=p1, t0=t0, tn=tn: e.tensor_tensor(t1_[:, 0:tn], p1[0:64, 0:tn], rc[:, t0:t0 + tn], ALU.mult), reads=[p1b, rB], writes=[tB])
                em.op("dve", lambda e, p2=p2, t0=t0, tn=tn: e.tensor_tensor(t2_[:, 0:tn], p2[0:64, 0:tn], rsn[:, t0:t0 + tn], ALU.mult), reads=[p2b, rB], writes=[tB])
                em.op("dve", lambda e, i=i, t0=t0, tn=tn: e.tensor_tensor(qrs[i][:, t0:t0 + tn], t1_[:, 0:tn], t2_[:, 0:tn], ALU.add), reads=[tB], writes=[qrsb[i]])
            em.dma("sp", QrT.ap()[h * 64:(h + 1) * 64, q_t0:T], qrs[i][:, q_t0:T], reads=[qrsb[i]])
        vst = [ar.alloc([128, 512], BF16, "vstE") for _ in range(3)]
        vstb = [Buf() for _ in range(3)]
        vi = 0
        for fc in range(4):
            w = wsl[fc % 3]
            wb = wslb[fc % 3]
            load_panel(w, wb, Wt("wkv_v", l)[0][:, fc * 512:(fc + 1) * 512], 4, 512, rb=Wt("wkv_v", l)[1])
            for tt in range(T // 128):
                pt, pb = ps_next()
                em.mm(pt[:, :], [(ckvn[:, k, tt * 128:(tt + 1) * 128], w[:, k, :]) for k in range(4)], [wb, nB], pb)
                s, sb_ = vst[vi % 3], vstb[vi % 3]
                if vi % 2 == 0:
                    em.op("dve", lambda e, s=s, pt=pt: e.tensor_copy(s[:], pt[:, :]), reads=[pb], writes=[sb_])
                else:
                    em.op("act", lambda e, s=s, pt=pt: e.activation(out=s[:], in_=pt[:, :], func=AF.Copy), reads=[pb], writes=[sb_])
                if tt < 2:
                    em.dma("sp", VC.ap()[tt * 128:(tt + 1) * 128, fc * 512:(fc + 1) * 512], s[:], reads=[sb_])
                else:
                    em.dma("sp", VL.ap()[(tt - 2) * 128:(tt - 1) * 128, fc * 512:(fc + 1) * 512], s[:], reads=[sb_])
                vi += 1
        em.barrier(engines=["pool"])
        ccb = Buf()
        for (a_in, a_out) in ((KnL, KnG8), (KrL, KrG8), (VL, VG8)):
            em.coll(a_in.ap().opt(), a_out.ap().opt(), writes=[ccb])
        em.barrier()
        selt = [ar.alloc([128, 8, 1024], BF16, "selt") for _ in range(2)]
        seltb = [Buf() for _ in range(2)]
        selo = [ar.alloc([128, 2, 1024], BF16, "selo") for _ in range(2)]
        selob = [Buf() for _ in range(2)]
        si_ = 0
        jobs = []
        for rt in range(16):
            jobs.append(([KnG8.ap()[r8 * 2048 + rt * 128:r8 * 2048 + (rt + 1) * 128, :] for r8 in range(8)],
                         [KnG.ap()[r2 * 2048 + rt * 128:r2 * 2048 + (rt + 1) * 128, :] for r2 in range(2)], 128))
        jobs.append(([KrG8.ap()[r8 * 64:(r8 + 1) * 64, :] for r8 in range(8)], [KrG.ap()[r2 * 64:(r2 + 1) * 64, :] for r2 in range(2)], 64))
        for tt in range(NLAT // 128):
            for hf in range(2):
                jobs.append(([VG8.ap()[r8 * NLAT + tt * 128:r8 * NLAT + (tt + 1) * 128, hf * 1024:(hf + 1) * 1024] for r8 in range(8)],
                             [VG.ap()[r2 * NLAT + tt * 128:r2 * NLAT + (tt + 1) * 128, hf * 1024:(hf + 1) * 1024] for r2 in range(2)], 128))
        for (srcs, dsts, npart) in jobs:
            i = si_ % 2
            si_ += 1
            for r8 in range(8):
                em.dma("sp", selt[i][0:npart, r8, :], srcs[r8], writes=[seltb[i]])
            for r2 in range(2):
                for r8 in range(8):
                    c_ = 8 + r2 * 8 + r8
                    if r8 == 0:
                        em.op("dve", lambda e, i=i, r2=r2, r8=r8, c_=c_, npart=npart: e.tensor_scalar(
                            selo[i][0:npart, r2, :], selt[i][0:npart, r8, :], sel[0:npart, c_:c_ + 1], None, ALU.mult),
                            reads=[seltb[i], cB], writes=[selob[i]])
                    else:
                        em.op("dve", lambda e, i=i, r2=r2, r8=r8, c_=c_, npart=npart: e.scalar_tensor_tensor(
                            selo[i][0:npart, r2, :], selt[i][0:npart, r8, :], sel[0:npart, c_:c_ + 1], selo[i][0:npart, r2, :], ALU.mult, ALU.add),
                            reads=[seltb[i], cB, selob[i]], writes=[selob[i]])
                em.dma("sp", dsts[r2], selo[i][0:npart, r2, :], reads=[selob[i]])
        ar.release(mE)
        if phase_end(f"mlaprep{l}"):
            return finish(), tap_out

        mF = ar.mark()
        NKL = 2 * NLAT
        NK = NCTX + NKL
        Kn = [ar.alloc([128, NK], BF16, "Kn") for _ in range(2)]
        Vh = [ar.alloc([128, NK // 128, 128], BF16, "Vh") for _ in range(2)]
        Qn = [ar.alloc([128, T], BF16, "Qn") for _ in range(2)]
        Qr = [ar.alloc([64, T], BF16, "Qr") for _ in range(2)]
        hB = [Buf() for _ in range(2)]
        Kr = ar.alloc([64, NK], BF16, "Kr")
        krB = Buf()
        em.dma("sp", Kr[:, 0:NCTX], KrC.ap(), writes=[krB])
        em.dma("sp", Kr[:, NCTX:NCTX + NLAT], KrG.ap()[0:64, :], writes=[krB])
        em.dma("sp", Kr[:, NCTX + NLAT:NK], KrG.ap()[64:128, :], writes=[krB])
        P = [ar.alloc([128, NK], BF16, "P") for _ in range(2)]
        PB = [Buf() for _ in range(2)]
        PT = [ar.alloc([128, NK // 128, 128], BF16, "PT") for _ in range(2)]
        PTB = [Buf() for _ in range(2)]
        stt = [ar.alloc([128, 4], F32, "stt") for _ in range(2)]
        sttb = [Buf() for _ in range(2)]
        otm = [ar.alloc([128, 128], BF16, "otm") for _ in range(2)]
        otmb = [Buf() for _ in range(2)]
        oTh = [ar.alloc([128, T], BF16, "oTh") for _ in range(2)]
        oThb = [Buf() for _ in range(2)]
        qi = 0
        for h in range(16):
            hi = h % 2
            rows = slice(h * 128, (h + 1) * 128)
            em.dma("sp", Kn[hi][:, 0:NCTX], KnC.ap()[rows, :], writes=[hB[hi]])
            em.dma("sp", Kn[hi][:, NCTX:NCTX + NLAT], KnG.ap()[h * 128:(h + 1) * 128, :], writes=[hB[hi]])
            em.dma("sp", Kn[hi][:, NCTX + NLAT:NK], KnG.ap()[2048 + h * 128:2048 + (h + 1) * 128, :], writes=[hB[hi]])
            em.dma("sp", Vh[hi][:, 0:2, :], VC.ap()[:, rows].rearrange("(c p) v -> p c v", p=128), writes=[hB[hi]])
            em.dma("sp", Vh[hi][:, 2:18, :], VG.ap()[:, rows].rearrange("(c p) v -> p c v", p=128), writes=[hB[hi]])
            em.dma("sp", Qn[hi][:, q_t0:T], QnT.ap()[rows, q_t0:T], writes=[hB[hi]])
            em.dma("sp", Qr[hi][:, q_t0:T], QrT.ap()[h * 64:(h + 1) * 64, q_t0:T], writes=[hB[hi]])
            for tt in out_tiles:
                is_ctx = tt < 2
                nk = NCTX if is_ctx else NK
                kch = tok_chunks(0, nk)
                pi_ = qi % 2
                qi += 1
                qs = slice(tt * 128, (tt + 1) * 128)
                banks = []
                for (k0, kn) in kch:
                    pt, pb = ps_next()
                    em.mm(pt[:, 0:kn], [(Qn[hi][:, qs], Kn[hi][:, k0:k0 + kn]), (Qr[hi][:, qs], Kr[:, k0:k0 + kn])], [hB[hi], krB], pb)
                    banks.append((pt, pb, k0, kn))
                st_ = stt[pi_]
                for bi, (pt, pb, k0, kn) in enumerate(banks):
                    em.op("dve", lambda e, pt=pt, kn=kn, bi=bi, st_=st_: e.reduce_max(st_[:, 0:1] if bi == 0 else st_[:, 1:2], pt[:, 0:kn], AX.X),
                          reads=[pb], writes=[sttb[pi_]])
                    if bi > 0:
                        em.op("dve", lambda e, st_=st_: e.tensor_tensor(st_[:, 0:1], st_[:, 0:1], st_[:, 1:2], ALU.max), reads=[sttb[pi_]], writes=[sttb[pi_]])
                em.op("dve", lambda e, st_=st_: e.tensor_scalar(st_[:, 1:2], st_[:, 0:1], -SCALE_MLA, None, ALU.mult), reads=[sttb[pi_]], writes=[sttb[pi_]])
                em.op("dve", lambda e, st_=st_: e.memset(st_[:, 2:3], 0.0), reads=[sttb[pi_]], writes=[sttb[pi_]])
                for bi, (pt, pb, k0, kn) in enumerate(banks):
                    em.op("act", lambda e, pt=pt, k0=k0, kn=kn, st_=st_, pi_=pi_: e.activation(
                        out=P[pi_][:, k0:k0 + kn], in_=pt[:, 0:kn], func=AF.Exp, scale=SCALE_MLA, bias=st_[:, 1:2], accum_out=st_[:, 3:4]),
                        reads=[pb, sttb[pi_]], writes=[PB[pi_], sttb[pi_]])
                    em.op("dve", lambda e, st_=st_: e.tensor_tensor(st_[:, 2:3], st_[:, 2:3], st_[:, 3:4], ALU.add), reads=[sttb[pi_]], writes=[sttb[pi_]])
                em.op("dve", lambda e, st_=st_: e.reciprocal(st_[:, 2:3], st_[:, 2:3]), reads=[sttb[pi_]], writes=[sttb[pi_]])
                nkt = nk // 128
                for g0 in range(0, nkt, 8):
                    g1 = min(nkt, g0 + 8)
                    pt, pb = ps_next()
                    ptb = pt[:].bitcast(BF16)
                    em.tr([(ptb[:, (kt - g0) * 128:(kt - g0 + 1) * 128], P[pi_][:, kt * 128:(kt + 1) * 128]) for kt in range(g0, g1)],
                          [PB[pi_], cB], pb, ident[:])
                    src = ptb[:, 0:(g1 - g0) * 128].rearrange("p (c q) -> p c q", q=128)
                    if (g0 // 8) % 2 == 0:
                        em.op("dve", lambda e, src=src, g0=g0, g1=g1, pi_=pi_: e.tensor_copy(PT[pi_][:, g0:g1, :], src), reads=[pb], writes=[PTB[pi_]])
                    else:
                        em.op("act", lambda e, src=src, g0=g0, g1=g1, pi_=pi_: e.activation(out=PT[pi_][:, g0:g1, :], in_=src, func=AF.Copy), reads=[pb], writes=[PTB[pi_]])
                po, pob = ps_next()
                if is_ctx:
                    pairs = [(PT[pi_][:, kt, :], Vh[hi][:, kt, :]) for kt in range(2)]
                else:
                    pairs = [(PT[pi_][:, kt, :], Vh[hi][:, kt, :]) for kt in range(18)]
                em.mm(po[:, 0:128], pairs, [PTB[pi_], hB[hi]], pob)
                em.op("dve", lambda e, po=po, st_=st_, pi_=pi_: e.tensor_scalar(otm[pi_][:], po[:, 0:128], st_[:, 2:3], None, ALU.mult),
                      reads=[pob, sttb[pi_]], writes=[otmb[pi_]])
                p3, p3b = ps_next()
                p3b16 = p3[:].bitcast(BF16)
                em.tr([(p3b16[:, 0:128], otm[pi_][:])], [otmb[pi_], cB], p3b, ident[:])
                em.op("act", lambda e, p3b16=p3b16, hi=hi, qs=qs: e.activation(out=oTh[hi][:, qs], in_=p3b16[:, 0:128], func=AF.Copy), reads=[p3b], writes=[oThb[hi]])
            em.dma("sp", mlaT.ap()[rows, q_t0:T], oTh[hi][:, q_t0:T], reads=[oThb[hi]])
        ar.release(mF)
        if phase_end(f"mla{l}"):
            return finish(), tap_out

        mG = ar.mark()
        actA = ar.alloc([128, KC, T], BF16, "actA")
        yT = ar.alloc([128, KC, T], BF16, "yT")
        aB = Buf()
        yB_ = Buf()
        wsl = [ar.alloc([128, KC, 512], BF16, "wslG") for _ in range(3)]
        wslb = [Buf() for _ in range(3)]
        gt = [ar.alloc([128, T], BF16, "gt") for _ in range(3)]
        gtb = [Buf() for _ in range(3)]
        tm_ = ar.alloc([128, 512], F32, "tmG")
        tmb = Buf()
        cntg = {"i": 0}
        for br, (srcT, wsrc, grow0) in enumerate(((glaT, "w_a", 4096), (mlaT, "w_b", 6144))):
            em.barrier()
            for k in range(KC):
                em.dma("sp", actA[:, k, q_t0:T], srcT.ap()[k * 128:(k + 1) * 128, q_t0:T], writes=[aB])

            def epi(f0, nf, t0, tn, pt, pb, br=br, grow0=grow0):
                kf = f0 // 128
                gi_ = cntg["i"] % 3
                if t0 == q_t0:
                    em.dma("sp", gt[gi_][:, q_t0:T], yA.ap()[grow0 + f0:grow0 + f0 + 128, q_t0:T], writes=[gtb[gi_]])
                if br == 0:
                    em.op("dve", lambda e: e.tensor_tensor(yT[:, kf, t0:t0 + tn], pt[:, 0:tn], gt[gi_][:, t0:t0 + tn], ALU.mult),
                          reads=[pb, gtb[gi_]], writes=[yB_])
                else:
                    em.op("dve", lambda e: e.tensor_tensor(tm_[:, 0:tn], pt[:, 0:tn], gt[gi_][:, t0:t0 + tn], ALU.mult),
                          reads=[pb, gtb[gi_]], writes=[tmb])
                    em.op("dve", lambda e: e.tensor_tensor(yT[:, kf, t0:t0 + tn], yT[:, kf, t0:t0 + tn], tm_[:, 0:tn], ALU.add),
                          reads=[tmb, yB_], writes=[yB_])
                if t0 + tn == T:
                    cntg["i"] += 1
            gemm_fm(actA, aB, KC, Wt(wsrc, l), D, out_chunks, epi, wsl, wslb)
        gab = ar.alloc([128, 2, D], F32, "gab")
        gabB = Buf()
        em.dma("sp", gab[:, 0, :], gaterow.ap()[l, 0, 0, :].partition_broadcast(128), writes=[gabB])
        em.dma("sp", gab[:, 1, :], gaterow.ap()[l, 0, 1, :].partition_broadcast(128), writes=[gabB])
        xt = [ar.alloc([128, 512], F32, "xtG") for _ in range(3)]
        xtb = [Buf() for _ in range(3)]
        xi = 0
        for fc in range(4):
            w = wsl[fc % 3]
            wb = wslb[fc % 3]
            load_panel(w, wb, Wt("w_o", l)[0][:, fc * 512:(fc + 1) * 512], KC, 512, rb=Wt("w_o", l)[1])
            for tt in out_tiles:
                s = 1 if tt < 2 else 0
                i = xi % 3
                xi += 1
                em.dma("sp", xt[i][:], xcur.ap()[tt * 128:(tt + 1) * 128, fc * 512:(fc + 1) * 512], writes=[xtb[i]])
                pt, pb = ps_next()
                em.mm(pt[:, :], [(yT[:, k, tt * 128:(tt + 1) * 128], w[:, k, :]) for k in range(KC)], [wb, yB_], pb)
                em.op("dve", lambda e, pt=pt, s=s, fc=fc: e.tensor_tensor(tm_[:], pt[:, :], gab[:, s, fc * 512:(fc + 1) * 512], ALU.mult),
                      reads=[pb, gabB], writes=[tmb])
                em.op("dve", lambda e, i=i: e.tensor_tensor(xt[i][:], xt[i][:], tm_[:], ALU.add), reads=[tmb, xtb[i]], writes=[xtb[i]])
                em.dma("sp", xres.ap()[tt * 128:(tt + 1) * 128, fc * 512:(fc + 1) * 512], xt[i][:], reads=[xtb[i]])
        if xcur is xin and q_t0 > 0:
            raise AssertionError("ctx rows must be carried")
        xcur = xres
        ar.release(mG)
        if phase_end(f"merge{l}"):
            return finish(), tap_out

        mH = ar.mark()
        hxT = ar.alloc([128, KC, T], BF16, "hxF")
        hb = norm_mod(l, 1, xres, hxT, out_tiles)
        em.barrier()
        HK = DFF // 128 // 2
        hT = ar.alloc([128, HK, T], BF16, "hT")
        hTB = Buf()
        wg = [ar.alloc([128, KC, 256], BF16, "wg") for _ in range(2)]
        wu = [ar.alloc([128, KC, 256], BF16, "wu") for _ in range(2)]
        wgb = [Buf() for _ in range(2)]
        wd = [ar.alloc([128, HK, 256], BF16, "wd") for _ in range(2)]
        wdb = [Buf() for _ in range(2)]
        sg = [ar.alloc([128, 512], F32, "sg") for _ in range(2)]
        sgb = [Buf() for _ in range(2)]
        gfb = ar.alloc([128, 2, D], F32, "gfb")
        gfB = Buf()
        em.dma("sp", gfb[:, 0, :], gaterow.ap()[l, 1, 0, :].partition_broadcast(128), writes=[gfB])
        em.dma("sp", gfb[:, 1, :], gaterow.ap()[l, 1, 1, :].partition_broadcast(128), writes=[gfB])
        xt = [ar.alloc([128, 256], F32, "xtH") for _ in range(3)]
        xtb = [Buf() for _ in range(3)]
        tm_ = ar.alloc([128, 256], F32, "tmH")
        tmb = Buf()
        pi = 0
        si = 0
        xi = 0
        for half in range(2):
            for pnl in range(HK // 2):
                c0_ = (half * HK) * 128 + pnl * 256
                i = pi % 2
                pi += 1
                load_panel(wg[i], wgb[i], Wt("w_fg", l)[0][:, c0_:c0_ + 256], KC, 256, rb=Wt("w_fg", l)[1])
                load_panel(wu[i], wgb[i], Wt("w_fu", l)[0][:, c0_:c0_ + 256], KC, 256, rb=Wt("w_fu", l)[1])
                for j in range(2):
                    kf = pnl * 2 + j
                    for (t0, tn) in out_chunks:
                        pg, pgb = ps_next()
                        em.mm(pg[:, 0:tn], [(wg[i][:, k, j * 128:(j + 1) * 128], hxT[:, k, t0:t0 + tn]) for k in range(KC)], [wgb[i], hb], pgb)
                        pu, pub = ps_next()
                        em.mm(pu[:, 0:tn], [(wu[i][:, k, j * 128:(j + 1) * 128], hxT[:, k, t0:t0 + tn]) for k in range(KC)], [wgb[i], hb], pub)
                        s_ = si % 2
                        si += 1
                        em.op("act", lambda e, pg=pg, tn=tn, s_=s_: e.activation(out=sg[s_][:, 0:tn], in_=pg[:, 0:tn], func=AF.Silu), reads=[pgb], writes=[sgb[s_]])
                        em.op("dve", lambda e, pu=pu, tn=tn, t0=t0, s_=s_, kf=kf: e.tensor_tensor(hT[:, kf, t0:t0 + tn], sg[s_][:, 0:tn], pu[:, 0:tn], ALU.mult),
                              reads=[pub, sgb[s_]], writes=[hTB])
            for fc in range(8):
                i = pi % 2
                pi += 1
                load_panel(wd[i], wdb[i], Wt("w_fd", l)[0][half * HK * 128:(half + 1) * HK * 128, fc * 256:(fc + 1) * 256], HK, 256, kgrp=6, rb=Wt("w_fd", l)[1])
                for tt in out_tiles:
                    s = 1 if tt < 2 else 0
                    xj = xi % 3
                    xi += 1
                    em.dma("sp", xt[xj][:], xres.ap()[tt * 128:(tt + 1) * 128, fc * 256:(fc + 1) * 256], writes=[xtb[xj]])
                    pt, pb = ps_next()
                    em.mm(pt[:, 0:256], [(hT[:, k, tt * 128:(tt + 1) * 128], wd[i][:, k, :]) for k in range(HK)], [wdb[i], hTB], pb)
                    em.op("dve", lambda e, pt=pt, s=s, fc=fc: e.tensor_tensor(tm_[:], pt[:, 0:256], gfb[:, s, fc * 256:(fc + 1) * 256], ALU.mult),
                          reads=[pb, gfB], writes=[tmb])
                    em.op("dve", lambda e, xj=xj: e.tensor_tensor(xt[xj][:], xt[xj][:], tm_[:], ALU.add), reads=[tmb, xtb[xj]], writes=[xtb[xj]])
                    em.dma("sp", xres.ap()[tt * 128:(tt + 1) * 128, fc * 256:(fc + 1) * 256], xt[xj][:], reads=[xtb[xj]])
            em.barrier()
        ar.release(mH)
        if phase_end(f"ffn{l}"):
            return finish(), tap_out

    mZ = ar.mark()
    gfin_b = ar.alloc([128, D], F32, "gfinb")
    gB2 = Buf()
    em.dma("sp", gfin_b[:], gfin.ap()[0, :].partition_broadcast(128), writes=[gB2])
    xt = [ar.alloc([128, D], F32, "xtZ") for _ in range(2)]
    xtb = [Buf() for _ in range(2)]
    junk = ar.alloc([128, D], BF16, "junkZ")
    jb = Buf()
    st = [ar.alloc([128, 4], F32, "stZ") for _ in range(2)]
    stb = [Buf() for _ in range(2)]
    for it, tt in enumerate(range(2, T // 128)):
        i = it % 2
        em.dma("sp", xt[i][:], xres.ap()[tt * 128:(tt + 1) * 128, :], writes=[xtb[i]])
        em.op("act", lambda e, i=i: e.activation(out=junk[:], in_=xt[i][:], func=AF.Square, accum_out=st[i][:, 0:1]), reads=[xtb[i]], writes=[jb, stb[i]])
        em.op("act", lambda e, i=i: e.activation(out=st[i][:, 1:2], in_=st[i][:, 0:1], func=AF.Sqrt, scale=1.0 / D, bias=eps_sb[:, 0:1]), reads=[stb[i], cB], writes=[stb[i]])
        em.op("dve", lambda e, i=i: e.reciprocal(st[i][:, 2:3], st[i][:, 1:2]), reads=[stb[i]], writes=[stb[i]])
        em.op("dve", lambda e, i=i: e.scalar_tensor_tensor(xt[i][:], xt[i][:], st[i][:, 2:3], gfin_b[:], ALU.mult, ALU.mult), reads=[xtb[i], stb[i], gB2], writes=[xtb[i]])
        em.dma("sp", yout.ap()[(tt - 2) * 128:(tt - 1) * 128, :], xt[i][:], reads=[xtb[i]])
    ar.release(mZ)
    return finish(), tap_out


def _fm(v):
    v = np.asarray(v, np.float32)
    return np.ascontiguousarray(v.reshape(-1, 128).T)


def _rope_tables(pos):
    pos = np.asarray(pos)
    row = (pos // 64).astype(np.float32)
    col = (pos % 64).astype(np.float32)
    n_pairs = 16
    freqs = (np.float32(10000.0) ** (-np.arange(n_pairs, dtype=np.float32) / np.float32(n_pairs))).astype(np.float32)
    ang = np.concatenate([row[:, None] * freqs, col[:, None] * freqs], axis=-1).astype(np.float32)
    cos = np.cos(ang).astype(np.float32).T
    sin = np.sin(ang).astype(np.float32).T
    C = np.concatenate([cos, cos], 0)
    S = np.concatenate([-sin, sin], 0)
    return C, S


DEINT = np.concatenate([np.arange(0, 64, 2), np.arange(1, 64, 2)])
SWAP = np.concatenate([np.arange(1, 64, 2), np.arange(0, 64, 2)])


def prepare_inputs(inp):
    f32 = np.float32
    x = np.asarray(inp["x"], f32)
    ctx = np.asarray(inp["ctx"], f32)
    c = np.asarray(inp["c"], f32)
    c_ctx = np.asarray(inp["c_ctx"], f32)
    w_in = np.asarray(inp["w_in"], f32)
    shared = {}
    cm = np.zeros((128, 256), f32)
    cm[:, 0:128] = np.eye(128, dtype=f32)
    jj, ii = np.meshgrid(np.arange(64), np.arange(64), indexing="ij")
    cm[0:64, 128:192] = (jj <= ii).astype(f32)
    cm[0:64, 192:256] = (jj >= ii).astype(f32)
    shared["cmat"] = cm
    rm = np.ones((128, T), f32)
    rm[:, ::CH] = 0.0
    shared["rmask"] = rm
    big = {"w_mod": np.asarray(inp["w_mod"], f32)}
    b_mod = np.asarray(inp["b_mod"], f32)
    shared["bmodF"] = np.concatenate([_fm(b_mod[l]) for l in range(L)], 1)
    shared["bmodrow"] = np.ascontiguousarray(np.broadcast_to(b_mod.reshape(1, -1), (2, L * 6 * D)))
    gv = [np.zeros((128, 0), f32)]
    for l in range(L):
        gv.append(_fm(inp["g_attn"][l]))
        gv.append(_fm(inp["g_ffn"][l]))
    gv.append(np.zeros((128, KC), f32))
    shared["gvec"] = np.ascontiguousarray(np.concatenate(gv, 1))
    shared["gfin"] = np.asarray(inp["g_final"], f32).reshape(1, D)
    big["w_in"] = w_in
    kr = w_in[:, :, OFF_KR:OFF_KR + 64]
    shared["w_kr2"] = np.ascontiguousarray(np.concatenate([kr[:, :, DEINT], kr[:, :, SWAP]], 2))
    shared["ggla"] = np.concatenate([_fm(inp["g_gla_out"][l]) for l in range(L)], 1)
    shared["gq"] = np.concatenate([_fm(inp["g_q_lora"][l]) for l in range(L)], 1)
    shared["gkv"] = np.concatenate([_fm(inp["g_kv_lora"][l]) for l in range(L)], 1)
    wq = np.asarray(inp["w_q_up"], f32).reshape(L, 512, 16, 192)
    big["wq_n"] = np.ascontiguousarray(wq[:, :, :, 0:128].reshape(L, 512, 2048))
    big["wq_r"] = np.ascontiguousarray(wq[:, :, :, 128:][:, :, :, DEINT].reshape(L, 512, 1024))
    big["wq_s"] = np.ascontiguousarray(wq[:, :, :, 128:][:, :, :, SWAP].reshape(L, 512, 1024))
    wkv = np.asarray(inp["w_kv_up"], f32).reshape(L, 512, 16, 256)
    big["wkv_n"] = np.ascontiguousarray(wkv[:, :, :, 0:128].reshape(L, 512, 2048))
    big["wkv_v"] = np.ascontiguousarray(wkv[:, :, :, 128:].reshape(L, 512, 2048))
    big["w_a"] = np.asarray(inp["w_branch_a"], f32)
    big["w_b"] = np.asarray(inp["w_branch_b"], f32)
    big["w_o"] = np.asarray(inp["w_out"], f32)
    big["w_fg"] = np.asarray(inp["w_ffn_gate"], f32)
    big["w_fu"] = np.asarray(inp["w_ffn_up"], f32)
    big["w_fd"] = np.asarray(inp["w_ffn_down"], f32)
    gfw = w_in[:, :, OFF_GF:OFF_GF + 16]
    gbw = w_in[:, :, OFF_GB:OFF_GB + 16]
    upf = np.asarray(inp["w_gla_up_f"], f32)
    upb = np.asarray(inp["w_gla_up_b"], f32)
    bf_ = np.asarray(inp["b_gla_f"], f32)
    bb_ = np.asarray(inp["b_gla_b"], f32)
    par = []
    for h in range(2):
        p = {}
        a, b_ = (gfw, gbw) if h == 0 else (gbw, gfw)
        p["w_g2"] = np.ascontiguousarray(np.concatenate([a, b_], 2))
        u0, u1 = (upf, upb) if h == 0 else (upb, upf)
        p["wup"] = np.ascontiguousarray(np.stack([u0, u1], 1))
        b0, b1 = (bf_, bb_) if h == 0 else (bb_, bf_)
        p["bup"] = np.ascontiguousarray(np.concatenate([np.concatenate([_fm(b0[l]), _fm(b1[l])], 1) for l in range(L)], 1))
        pos = np.arange(NLAT) if h == 0 else (2 * NLAT - 1 - np.arange(NLAT))
        C, S = _rope_tables(pos)
        p["ropeC"] = np.ascontiguousarray(np.concatenate([np.ones((64, NCTX), f32), C], 1))
        p["ropeS"] = np.ascontiguousarray(np.concatenate([np.zeros((64, NCTX), f32), S], 1))
        par.append(p)
    in_maps = []
    for core in range(8):
        b, h = core // 2, core % 2
        m = dict(shared)
        m.update(par[h])
        for nm, arr in big.items():
            for l in range(L):
                rows = arr.shape[1] // 8
                m[f"{nm}_{l}_sh"] = np.ascontiguousarray(arr[l, core * rows:(core + 1) * rows])
        if h == 0:
            xl = x[b, 0:NLAT]
            cl = ctx[b]
        else:
            xl = x[b, ::-1][0:NLAT]
            cl = ctx[b, ::-1]
        m["xin"] = np.ascontiguousarray(np.concatenate([cl, xl], 0))
        sl = np.zeros((128, 24), f32)
        sl[:, core ^ 1] = 1.0
        sl[:, 8 + 2 * b] = 1.0
        sl[:, 16 + 2 * b + 1] = 1.0
        m["selp"] = sl
        cv = np.stack([_fm(c[b]), _fm(c_ctx)], 2)
        m["cvecs"] = np.ascontiguousarray(cv)
        in_maps.append(m)
    return in_maps


_NC_CACHE = {}


def kernel(**inputs):
    in_maps = prepare_inputs(inputs)
    if "nc" not in _NC_CACHE:
        _NC_CACHE["nc"] = build()[0]
    nc = _NC_CACHE["nc"]
    res = run_bass_kernel_spmd(nc, in_maps, core_ids=list(range(8)))
    B = inputs["x"].shape[0]
    out = np.zeros((B, 2 * NLAT, D), np.float32)
    for core in range(8):
        b, h = core // 2, core % 2
        y = np.asarray(res.results[core]["yout"], np.float32)
        if h == 0:
            out[b, 0:NLAT] = y
        else:
            out[b, NLAT:] = y[::-1]
    return out
```
